# Optimizing a Trainium2 kernel written in Bass

```python
import jax, jax.numpy as jnp
from jax import lax
import numpy as np

D_MODEL = 2048
BATCH = 8
SEQ = 2048
DEPTH = 1

CHUNK = 64
LEFT_CHUNKS = 8
BAND_CHUNKS = LEFT_CHUNKS + 1
BAND = BAND_CHUNKS * CHUNK
MIX_WIDTH = D_MODEL
CONV_WIDTH = MIX_WIDTH // 2
ATTN_WIDTH = MIX_WIDTH - CONV_WIDTH
HEAD_DIM = 128
N_HEADS = ATTN_WIDTH // HEAD_DIM
CONV_KERNEL = 31
MAX_REL_DIST = 128
N_REL = 2 * MAX_REL_DIST + 1
D_FF = 4 * D_MODEL
IN_COLS = 2 * CONV_WIDTH + 3 * ATTN_WIDTH
EPS = 1e-6

kernel_name = "hybrid_conformer_conv_chunked_relbias_attn_block"


def rms_norm(x, g):
    xf = x.astype(jnp.float32)
    y = xf * lax.rsqrt(jnp.mean(xf * xf, axis=-1, keepdims=True) + EPS)
    return (y * g.astype(jnp.float32)).astype(x.dtype)


def layer_norm(x, g, b):
    xf = x.astype(jnp.float32)
    mu = jnp.mean(xf, axis=-1, keepdims=True)
    xc = xf - mu
    var = jnp.mean(xc * xc, axis=-1, keepdims=True)
    y = xc * lax.rsqrt(var + EPS) * g.astype(jnp.float32) + b.astype(jnp.float32)
    return y.astype(x.dtype)


def conv_mixer(val, gate, w_dw, b_dw, ln_g, ln_b):
    h = val * jax.nn.sigmoid(gate)
    h_pad = jnp.pad(h, ((0, 0), (CONV_KERNEL - 1, 0), (0, 0)))
    y = lax.conv_general_dilated(
        h_pad, w_dw[:, None, :].astype(h.dtype), window_strides=(1,), padding="VALID",
        dimension_numbers=("NWC", "WIO", "NWC"), feature_group_count=CONV_WIDTH)
    y = y + b_dw
    y = layer_norm(y, ln_g, ln_b)
    return jax.nn.silu(y)


def chunked_relbias_attention(q, k, v, q_g, k_g, rel_table):
    b, s, _ = q.shape
    nc = s // CHUNK
    q = rms_norm(q.reshape(b, s, N_HEADS, HEAD_DIM), q_g)
    k = rms_norm(k.reshape(b, s, N_HEADS, HEAD_DIM), k_g)
    v = v.reshape(b, s, N_HEADS, HEAD_DIM)
    q_c = q.reshape(b, nc, CHUNK, N_HEADS, HEAD_DIM)
    pad = ((0, 0), (LEFT_CHUNKS, 0), (0, 0), (0, 0), (0, 0))
    k_pad = jnp.pad(k.reshape(b, nc, CHUNK, N_HEADS, HEAD_DIM), pad)
    v_pad = jnp.pad(v.reshape(b, nc, CHUNK, N_HEADS, HEAD_DIM), pad)
    k_band = jnp.concatenate([k_pad[:, j:j + nc] for j in range(BAND_CHUNKS)], axis=2)
    v_band = jnp.concatenate([v_pad[:, j:j + nc] for j in range(BAND_CHUNKS)], axis=2)

    scores = jnp.einsum("bcqhd,bckhd->bhcqk", q_c, k_band).astype(jnp.float32)
    scores = scores * (HEAD_DIM ** -0.5)

    q_local = np.arange(CHUNK)[:, None]
    k_local = np.arange(BAND)[None, :]
    rel = q_local + LEFT_CHUNKS * CHUNK - k_local
    rel_idx = np.clip(rel, -MAX_REL_DIST, MAX_REL_DIST) + MAX_REL_DIST
    bias = rel_table.astype(jnp.float32)[:, rel_idx]
    scores = scores + bias[None, :, None, :, :]

    chunk_ids = jnp.arange(nc)[:, None]
    band_chunk = jnp.arange(BAND)[None, :] // CHUNK
    valid = (chunk_ids + band_chunk - LEFT_CHUNKS) >= 0
    scores = jnp.where(valid[None, None, :, None, :], scores, jnp.finfo(jnp.float32).min)

    p = jax.nn.softmax(scores, axis=-1).astype(v.dtype)
    out = jnp.einsum("bhcqk,bckhd->bcqhd", p, v_band)
    return out.reshape(b, s, ATTN_WIDTH)


def setup_inputs(seed: int = 0) -> dict:
    key = jax.random.key(seed)
    ks = jax.random.split(key, 16)
    nrm = jax.random.normal
    f32 = jnp.float32
    return {
        "x": nrm(ks[0], (BATCH, SEQ, D_MODEL), f32),
        "ln1_g": 1.0 + 0.1 * nrm(ks[1], (DEPTH, D_MODEL), f32),
        "w_in": nrm(ks[2], (DEPTH, D_MODEL, IN_COLS), f32) * D_MODEL ** -0.5,
        "w_dw": nrm(ks[3], (DEPTH, CONV_KERNEL, CONV_WIDTH), f32) * CONV_KERNEL ** -0.5,
        "b_dw": 0.02 * nrm(ks[4], (DEPTH, CONV_WIDTH), f32),
        "conv_ln_g": 1.0 + 0.1 * nrm(ks[5], (DEPTH, CONV_WIDTH), f32),
        "conv_ln_b": 0.02 * nrm(ks[6], (DEPTH, CONV_WIDTH), f32),
        "q_norm_g": 1.0 + 0.1 * nrm(ks[7], (DEPTH, HEAD_DIM), f32),
        "k_norm_g": 1.0 + 0.1 * nrm(ks[8], (DEPTH, HEAD_DIM), f32),
        "rel_bias": 0.2 * nrm(ks[9], (DEPTH, N_HEADS, N_REL), f32),
        "out_norm_conv_g": 1.0 + 0.1 * nrm(ks[10], (DEPTH, CONV_WIDTH), f32),
        "out_norm_attn_g": 1.0 + 0.1 * nrm(ks[11], (DEPTH, ATTN_WIDTH), f32),
        "w_out": nrm(ks[12], (DEPTH, MIX_WIDTH, D_MODEL), f32) * MIX_WIDTH ** -0.5,
        "ln2_g": 1.0 + 0.1 * nrm(ks[13], (DEPTH, D_MODEL), f32),
        "w_ff1": nrm(ks[14], (DEPTH, D_MODEL, D_FF), f32) * D_MODEL ** -0.5,
        "w_ff2": nrm(ks[15], (DEPTH, D_FF, D_MODEL), f32) * D_FF ** -0.5,
    }


def reference(x, ln1_g, w_in, w_dw, b_dw, conv_ln_g, conv_ln_b, q_norm_g, k_norm_g,
              rel_bias, out_norm_conv_g, out_norm_attn_g, w_out, ln2_g, w_ff1, w_ff2):
    split_points = [CONV_WIDTH, 2 * CONV_WIDTH, 2 * CONV_WIDTH + ATTN_WIDTH,
                    2 * CONV_WIDTH + 2 * ATTN_WIDTH]
    for l in range(DEPTH):
        h = rms_norm(x, ln1_g[l])
        u = jnp.einsum("bsd,de->bse", h, w_in[l])
        c_val, c_gate, q, k, v = jnp.split(u, split_points, axis=-1)
        y_conv = conv_mixer(c_val, c_gate, w_dw[l], b_dw[l], conv_ln_g[l], conv_ln_b[l])
        y_attn = chunked_relbias_attention(q, k, v, q_norm_g[l], k_norm_g[l], rel_bias[l])
        y = jnp.concatenate([rms_norm(y_conv, out_norm_conv_g[l]),
                             rms_norm(y_attn, out_norm_attn_g[l])], axis=-1)
        x = x + jnp.einsum("bse,ed->bsd", y, w_out[l])
        h2 = rms_norm(x, ln2_g[l])
        f = jax.nn.relu(jnp.einsum("bsd,df->bsf", h2, w_ff1[l]))
        x = x + jnp.einsum("bsf,fd->bsd", f * f, w_ff2[l])
    return x
```

```python
from contextlib import ExitStack
import numpy as np
import concourse.bass as bass
import concourse.mybir as mybir
from concourse.bass_utils import run_bass_kernel_spmd

F32 = mybir.dt.float32
BF16 = mybir.dt.bfloat16
AF = mybir.ActivationFunctionType
ALU = mybir.AluOpType

S = 2048
D = 2048
NT = 16
DFF = 8192
EPS = 1e-6
NPV = 74 + 8 * 31

ENGS = ("pe", "act", "dve", "pool", "sp")
SEM_EPOCH = 30000


class Op:
    __slots__ = ("eng", "fn", "idx", "signal", "waits", "dma_sem", "dma_val", "is_dma")

    def __init__(self, eng, fn, idx):
        self.eng = eng
        self.fn = fn
        self.idx = idx
        self.signal = False
        self.waits = []
        self.dma_sem = None
        self.dma_val = 0
        self.is_dma = False


class Prog:
    def __init__(self, same_engine_sync=True):
        self.ops = {e: [] for e in ENGS}
        self.last_w = {}
        self.readers = {}
        self.waited = {e: {p: -1 for p in ENGS} for e in ENGS}
        self.dma_waited = {e: {} for e in ENGS}
        self.dma_sems = {}
        self.same_engine_sync = same_engine_sync
        self.all_dma = []

    def _add_wait(self, op, dep):
        if dep is None or dep is op:
            return
        e = op.eng
        if dep.is_dma:
            cur = self.dma_waited[e].get(dep.dma_sem, 0)
            if cur >= dep.dma_val:
                return
            self.dma_waited[e][dep.dma_sem] = dep.dma_val
            op.waits.append(dep)
            return
        if dep.eng == e:
            if e == "pe" or not self.same_engine_sync:
                return
        if self.waited[e][dep.eng] >= dep.idx:
            return
        self.waited[e][dep.eng] = dep.idx
        dep.signal = True
        op.waits.append(dep)

    def _deps(self, op, reads, writes):
        for r in reads:
            self._add_wait(op, self.last_w.get(r))
        for w in writes:
            self._add_wait(op, self.last_w.get(w))
            for rd in self.readers.get(w, ()):
                self._add_wait(op, rd)
        for r in reads:
            self.readers.setdefault(r, []).append(op)
        for w in writes:
            self.last_w[w] = op
            self.readers[w] = []

    def op(self, eng, fn, reads=(), writes=()):
        o = Op(eng, fn, len(self.ops[eng]))
        self._deps(o, reads, writes)
        self.ops[eng].append(o)
        return o

    def dma(self, eng, semkey, fn, reads=(), writes=()):
        o = Op(eng, fn, len(self.ops[eng]))
        o.is_dma = True
        c = self.dma_sems.setdefault(semkey, [0])
        c[0] += 16
        o.dma_sem = semkey
        o.dma_val = c[0]
        self._deps(o, reads, writes)
        self.ops[eng].append(o)
        self.all_dma.append(o)
        return o

    def barrier(self):
        lastc = {}
        for e in ENGS:
            lastc[e] = None
            for o in reversed(self.ops[e]):
                if not o.is_dma and o.fn is not None:
                    lastc[e] = o
                    break
        latest_dma = {}
        for o in self.all_dma:
            latest_dma[o.dma_sem] = o
        for e in ENGS:
            o = Op(e, None, len(self.ops[e]))
            for p in ENGS:
                if p != e:
                    self._add_wait(o, lastc[p])
            for d in latest_dma.values():
                self._add_wait(o, d)
            self.ops[e].append(o)
        self.last_w = {}
        self.readers = {}

    def wait_all_dma(self, eng):
        latest = {}
        for o in self.all_dma:
            latest[o.dma_sem] = o
        o = Op(eng, None, len(self.ops[eng]))
        for d in latest.values():
            self._add_wait(o, d)
        self.ops[eng].append(o)

    def emit(self, nc):
        with ExitStack() as st:
            esems = {}
            for e in ENGS:
                nsig = sum(1 for o in self.ops[e] if o.signal and not o.is_dma)
                n = max(1, (nsig + SEM_EPOCH - 1) // SEM_EPOCH)
                esems[e] = [st.enter_context(nc.semaphore(f"s_{e}_{i}")) for i in range(n)]
            dsems = {}
            for i, k in enumerate(self.dma_sems):
                dsems[k] = st.enter_context(nc.semaphore(f"d_{i}"))
            sig = {}
            for e in ENGS:
                c = 0
                for o in self.ops[e]:
                    if o.signal and not o.is_dma:
                        sig[o] = (esems[e][c // SEM_EPOCH], c % SEM_EPOCH + 1)
                        c += 1
            block = st.enter_context(nc.Block())

            def run(e, eng):
                for o in self.ops[e]:
                    for d in o.waits:
                        if d.is_dma:
                            eng.wait_ge(dsems[d.dma_sem], d.dma_val)
                        else:
                            s, v = sig[d]
                            eng.wait_ge(s, v)
                    if o.fn is None:
                        continue
                    ins = o.fn(eng)
                    if o.is_dma:
                        ins.then_inc(dsems[o.dma_sem], 16)
                    elif o.signal:
                        s, v = sig[o]
                        ins.then_inc(s, 1)

            @block.tensor
            def _(eng):
                run("pe", eng)

            @block.scalar
            def _(eng):
                run("act", eng)

            @block.vector
            def _(eng):
                run("dve", eng)

            @block.gpsimd
            def _(eng):
                run("pool", eng)

            @block.sync
            def _(eng):
                run("sp", eng)


def build_program(stage="full", same_engine_sync=True):
    nc = bass.Bass("TRN2", target_bir_lowering=False)
    x_d = nc.dram_tensor("x", [S, D], F32, kind="ExternalInput")
    win_d = nc.dram_tensor("w_in", [D, 5120], F32, kind="ExternalInput")
    wout_d = nc.dram_tensor("w_out", [D, D], F32, kind="ExternalInput")
    w1_d = nc.dram_tensor("w_ff1", [D, DFF], F32, kind="ExternalInput")
    w2_d = nc.dram_tensor("w_ff2", [DFF, D], F32, kind="ExternalInput")
    pv_d = nc.dram_tensor("pvec", [128, NPV], F32, kind="ExternalInput")
    relb_d = nc.dram_tensor("relb", [8, 128, 640], F32, kind="ExternalInput")
    id_d = nc.dram_tensor("ident", [128, 128], F32, kind="ExternalInput")
    out_d = nc.dram_tensor("out", [S, D], F32, kind="ExternalOutput")
    dbg_d = None
    if stage != "full":
        dbg_d = nc.dram_tensor("dbg", [128, 16, 2048], BF16, kind="ExternalOutput")
        dbg2_d = nc.dram_tensor("dbg2", [128, 16], F32, kind="ExternalOutput")

    P = Prog(same_engine_sync=same_engine_sync)
    with ExitStack() as st:
        def sb(name, shape, dt):
            return st.enter_context(nc.sbuf_tensor(name, shape, dt))

        R0 = sb("R0", [128, 16, 2048], BF16)
        R1 = sb("R1", [128, 16, 2048], BF16)
        R2 = sb("R2", [128, 18944], F32)
        pv = sb("pv", [128, NPV], F32)
        ident = sb("identf", [128, 128], F32)
        identb = sb("identb", [128, 128], BF16)
        onesf = sb("onesf", [128, 128], F32)
        onesb = sb("onesb", [128, 128], BF16)
        ssa = sb("ssa", [128, 16], F32)
        ra = sb("ra", [128, 16], F32)
        sscol = sb("sscol", [128, 16], F32)
        rcol = sb("rcol", [128, 16], F32)
        tcol = sb("tcol", [128, 16], F32)
        halo = sb("halo", [128, 8, 32], BF16)
        epsc = sb("epsc", [128, 1], F32)
        psb = [st.enter_context(nc.psum_tensor(f"ps{i}", [128, 512], F32)) for i in range(8)]

        R2b = R2.bitcast(BF16)
        R1f = R1.bitcast(F32).reshape([128, 8, 2048])

        hT = R0
        yT = R1

        class Arena:
            def __init__(self):
                self.off = 0

            def f32(self, n):
                o = self.off
                self.off += n
                assert self.off <= 18944, self.off
                return o

            def bf(self, n):
                o = self.off
                self.off += (n + 1) // 2
                assert self.off <= 18944, self.off
                return 2 * o

        def PS(b):
            return ("ps", b)

        P.dma("sp", "pv", lambda e: e.dma_start(out=pv[:, :], in_=pv_d.ap()), writes=["pv"])
        P.dma("sp", "id", lambda e: e.dma_start(out=ident[:, :], in_=id_d.ap()), writes=["ident"])
        P.op("dve", lambda e: e.tensor_copy(out=identb[:, :], in_=ident[:, :]), reads=["ident"], writes=["identb"])
        P.op("pool", lambda e: e.memset(onesf[:, :], 1.0), writes=["onesf"])
        P.op("pool", lambda e: e.memset(onesb[:, :], 1.0), writes=["onesb"])
        P.op("pool", lambda e: e.memset(ssa[:, :], 0.0), writes=["ssa"])
        P.op("pool", lambda e: e.memset(epsc[:, :], EPS), writes=["epsc"])
        P.op("pool", lambda e: e.memset(sscol[:, :], 0.0), writes=["sscol"])
        P.barrier()

        def phase_norm_T(src_d, gcol0, dstT, tag):
            ar = Arena()
            NXB = 4
            o_x = [ar.f32(2048) for _ in range(NXB)]
            o_g = ar.f32(2048)
            o_j = ar.bf(2048)
            xt = [R2[:, o_x[i]:o_x[i] + 2048] for i in range(NXB)]
            Gb = R2[:, o_g:o_g + 2048].rearrange("p (c f) -> p c f", f=128)
            junk = R2b[:, o_j:o_j + 2048]
            o_dg = [ar.f32(128) for _ in range(NXB)]
            dg = [R2[:, o:o + 128] for o in o_dg]
            for c in range(16):
                P.op("dve", lambda e, c=c: e.tensor_scalar(out=Gb[:, c, :], in0=onesf[:, :], scalar1=pv[:, gcol0 + c:gcol0 + c + 1], scalar2=None, op0=ALU.mult), writes=[("Gb", c)])
            P.op("dve", lambda e: e.memset(sscol[:, :], 0.0), writes=["sscol"])
            def load(tt):
                if tt >= NT:
                    return
                i = tt % NXB
                P.dma("sp", f"xt{i}", lambda e: e.dma_start(out=xt[i], in_=src_d.ap()[tt * 128:(tt + 1) * 128, :]), writes=[("xt", i)])

            def stage1(tt):
                i = tt % NXB
                P.op("act", lambda e: e.activation(out=junk, in_=xt[i], func=AF.Square, accum_out=sscol[:, tt:tt + 1]), reads=[("xt", i), "sscol"], writes=["junk", ("ss", tt)])
                P.op("act", lambda e: e.activation(out=tcol[:, tt:tt + 1], in_=sscol[:, tt:tt + 1], func=AF.Ln, scale=1.0 / D, bias=epsc[:, 0:1]), reads=[("ss", tt)], writes=[("tc", tt)])
                P.op("act", lambda e: e.activation(out=rcol[:, tt:tt + 1], in_=tcol[:, tt:tt + 1], func=AF.Exp, scale=-0.5), reads=[("tc", tt)], writes=[("rc", tt)])
                P.op("act", lambda e: e.activation(out=xt[i], in_=xt[i], func=AF.Copy, scale=rcol[:, tt:tt + 1]), reads=[("rc", tt)], writes=[("xt", i)])

            def stage2(tt):
                i = tt % NXB
                b0 = 4 * (tt % 2)
                for g in range(4):
                    pT = psb[b0 + g][:, :].rearrange("p (j f) -> p j f", f=128)
                    for j in range(4):
                        c = g * 4 + j
                        P.op("pe", lambda e, pT=pT, j=j, c=c: e.transpose(out=pT[:, j, :], in_=xt[i][:, c * 128:(c + 1) * 128], identity=ident[:, :]), reads=[("xt", i)], writes=[PS(b0 + g)])
                    P.op("dve", lambda e, pT=pT, g=g: e.tensor_tensor(out=dstT[:, g * 4:(g + 1) * 4, tt * 128:(tt + 1) * 128], in0=pT, in1=Gb[:, g * 4:(g + 1) * 4, :], op=ALU.mult), reads=[("Gb", c) for c in range(g * 4, g * 4 + 4)], writes=[PS(b0 + g)])

            for tt in range(NXB - 1):
                load(tt)
            stage1(0)
            for tt in range(NT):
                load(tt + NXB - 1)
                if tt + 1 < NT:
                    stage1(tt + 1)
                stage2(tt)
            P.barrier()

        phase_norm_T(x_d, 0, hT, "A")

        def dump(t3, small=None):
            P.barrier()
            P.dma("sp", "dbg", lambda e: e.dma_start(out=dbg_d.ap(), in_=t3[:, :, :]))
            if small is not None:
                P.dma("sp", "dbg2", lambda e: e.dma_start(out=dbg2_d.ap(), in_=small[:, :]))
            P.wait_all_dma("sp")
            P.emit(nc)
            return nc

        if stage == "A":
            return dump(hT, rcol)

        def phase_conv():
            ar = Arena()
            o_w = [ar.bf(4096) for _ in range(4)]
            o_G = [ar.bf(544) for _ in range(2)]
            o_D = [ar.bf(31 * 128) for _ in range(2)]
            o_sig = [ar.f32(512) for _ in range(2)]
            o_tmp = [ar.bf(512) for _ in range(2)]
            o_t1 = [ar.f32(512) for _ in range(2)]
            o_mu = ar.f32(512)
            o_rs = ar.f32(512)
            o_m2 = ar.f32(512)
            wsl = [R2b[:, o:o + 4096].rearrange("p (k c) -> p k c", c=256) for o in o_w]
            Gbuf = [R2b[:, o:o + 544] for o in o_G]
            Dks = [R2b[:, o:o + 31 * 128].rearrange("p (k c) -> p k c", c=128) for o in o_D]
            sig = [R2[:, o:o + 512] for o in o_sig]
            tmp = [R2b[:, o:o + 512] for o in o_tmp]
            t1 = [R2[:, o:o + 512] for o in o_t1]
            mu = R2[:, o_mu:o_mu + 512]
            rs = R2[:, o_rs:o_rs + 512]
            m2 = R2[:, o_m2:o_m2 + 512]
            ycq = [R1f[:, 4 + 2 * i:6 + 2 * i, :].rearrange("p a (b f) -> p (a b) f", f=512) for i in range(2)]
            cnt = [0]

            def load_w(slot, col0):
                for q in range(4):
                    P.dma("pool", f"cw{slot}", lambda e, q=q: e.dma_start(out=wsl[slot][:, 4 * q:4 * q + 4, :], in_=win_d.ap()[512 * q:512 * q + 512, col0:col0 + 256].rearrange("(k p) c -> p k c", p=128)), writes=[("cw", slot, q)])

            seq = [(qd, cc) for qd in range(4) for cc in range(8)]

            pend = [None]

            def main_gen(qd):
                yc = ycq[qd % 2]
                t0 = qd * 512
                for cc in range(8):
                    si = qd * 8 + cc
                    pi = si // 2
                    sv = (2 * pi) % 4
                    sg = sv + 1
                    wc = (cc % 2) * 128
                    if cc % 2 == 0 and pi + 1 < len(seq) // 2:
                        ncc = seq[2 * (pi + 1)][1]
                        load_w((2 * pi + 2) % 4, ncc * 128)
                        load_w((2 * pi + 3) % 4, 1024 + ncc * 128)
                    gi = si % 2
                    G = Gbuf[gi]
                    Dk = Dks[gi]
                    bv, bg, bc = si % 2, 2 + si % 2, 4 + si % 2
                    P.op("dve", lambda e, cc=cc, Dk=Dk: e.tensor_tensor(out=Dk, in0=bass.AP(identb, 0, [[128, 128], [0, 31], [1, 128]]), in1=bass.AP(pv, 74 + cc * 31, [[NPV, 128], [1, 31], [0, 128]]), op=ALU.mult), writes=[("Dk", gi)])
                    if qd == 0:
                        P.op("dve", lambda e, G=G: e.memset(G[:, 0:30], 0.0), writes=[("G", gi)])
                    else:
                        P.op("dve", lambda e, G=G, cc=cc: e.tensor_copy(out=G[:, 0:30], in_=halo[:, cc, 0:30]), reads=[("halo", cc)], writes=[("G", gi)])
                    for k in range(16):
                        P.op("pe", lambda e, k=k, bv=bv, sv=sv, wc=wc: e.matmul(psb[bv][:, :], lhsT=wsl[sv][:, k, wc:wc + 128], rhs=hT[:, k, t0:t0 + 512], start=(k == 0), stop=(k == 15)), reads=[("cw", sv, q) for q in (3, 2, 1, 0)], writes=[PS(bv)])
                    for k in range(16):
                        P.op("pe", lambda e, k=k, bg=bg, sg=sg, wc=wc: e.matmul(psb[bg][:, :], lhsT=wsl[sg][:, k, wc:wc + 128], rhs=hT[:, k, t0:t0 + 512], start=(k == 0), stop=(k == 15)), reads=[("cw", sg, q) for q in (3, 2, 1, 0)], writes=[PS(bg)])
                    P.op("act", lambda e, bg=bg, gi=gi: e.activation(out=sig[gi], in_=psb[bg][:, :], func=AF.Sigmoid), writes=[PS(bg), ("sig", gi)])
                    P.op("dve", lambda e, bv=bv, gi=gi, G=G: e.tensor_tensor(out=G[:, 30:542], in0=psb[bv][:, :], in1=sig[gi], op=ALU.mult), reads=[("sig", gi)], writes=[PS(bv), ("G", gi)])
                    if qd < 3:
                        P.op("dve", lambda e, G=G, cc=cc: e.tensor_copy(out=halo[:, cc, 0:30], in_=G[:, 512:542]), reads=[("G", gi)], writes=[("halo", cc)])
                    if pend[0] is not None:
                        pend[0]()
                    yield

                    def conv(bc=bc, G=G, Dk=Dk, gi=gi, cc=cc):
                        for k in range(31):
                            P.op("pe", lambda e, k=k: e.matmul(psb[bc][:, :], lhsT=Dk[:, k, :], rhs=G[:, k:k + 512], start=(k == 0), stop=(k == 30)), reads=[("Dk", gi), ("G", gi)], writes=[PS(bc)])
                        P.op("act", lambda e: e.activation(out=yc[:, cc, :], in_=psb[bc][:, :], func=AF.Identity, bias=pv[:, 32 + cc:33 + cc]), writes=[PS(bc), ("yc", qd % 2, cc)])
                    pend[0] = conv
                pend[0]()
                pend[0] = None
                yield

            def fin_gen(qd):
                yc = ycq[qd % 2]
                yb = qd % 2
                t0 = qd * 512
                ycs = [yc[:, c, :] for c in range(8)]
                for c in range(8):
                    P.op("pe", lambda e, c=c: e.matmul(psb[6][:, :], lhsT=onesf[:, :], rhs=ycs[c], start=(c == 0), stop=(c == 7)), reads=[("yc", yb, c)], writes=[PS(6)])
                    if c % 2 == 1:
                        yield
                lag = None
                for c in range(8):
                    ti = cnt[0] % 2
                    cnt[0] += 1
                    P.op("act", lambda e, c=c, ti=ti: e.activation(out=tmp[ti], in_=ycs[c], func=AF.Square), reads=[("yc", yb, c)], writes=[("tmp", ti)])
                    yield
                    if lag is not None:
                        lag()
                    lag = (lambda c=c, ti=ti: P.op("pe", lambda e: e.matmul(psb[7][:, :], lhsT=onesb[:, :], rhs=tmp[ti], start=(c == 0), stop=(c == 7)), reads=[("tmp", ti)], writes=[PS(7)]))
                yield
                lag()
                P.op("act", lambda e: e.activation(out=mu, in_=psb[6][:, :], func=AF.Identity, scale=1.0 / 1024), writes=[PS(6), "mu"])
                P.op("dve", lambda e: e.tensor_tensor(out=m2, in0=mu, in1=mu, op=ALU.mult), reads=["mu"], writes=["m2"])
                P.op("dve", lambda e: e.scalar_tensor_tensor(out=rs, in0=psb[7][:, :], scalar=1.0 / 1024, in1=m2, op0=ALU.mult, op1=ALU.subtract), reads=["m2"], writes=[PS(7), "rs"])
                P.op("act", lambda e: e.activation(out=rs, in_=rs, func=AF.Ln, bias=epsc[:, 0:1]), writes=["rs"])
                P.op("act", lambda e: e.activation(out=rs, in_=rs, func=AF.Exp, scale=-0.5), writes=["rs"])
                yield
                lag2 = [None]
                for c in range(8):
                    ti = c % 2
                    P.op("dve", lambda e, c=c, ti=ti: e.tensor_tensor(out=t1[ti], in0=ycs[c], in1=mu, op=ALU.subtract), reads=[("yc", yb, c), "mu"], writes=[("t1", ti)])
                    P.op("dve", lambda e, ti=ti: e.tensor_tensor(out=t1[ti], in0=t1[ti], in1=rs, op=ALU.mult), reads=["rs"], writes=[("t1", ti)])
                    P.op("act", lambda e, c=c, ti=ti: e.activation(out=ycs[c], in_=t1[ti], func=AF.Silu, scale=pv[:, 40 + c:41 + c], bias=pv[:, 48 + c:49 + c]), reads=[("t1", ti)], writes=[("yc", yb, c)])
                    tj = cnt[0] % 2
                    cnt[0] += 1
                    P.op("act", lambda e, c=c, tj=tj: e.activation(out=tmp[tj], in_=ycs[c], func=AF.Square), reads=[("yc", yb, c)], writes=[("tmp", tj)])
                    yield
                    if lag2[0] is not None:
                        lag2[0]()
                    lag2[0] = (lambda c=c, tj=tj: P.op("pe", lambda e: e.matmul(psb[6][:, :], lhsT=onesb[:, :], rhs=tmp[tj], start=(c == 0), stop=(c == 7)), reads=[("tmp", tj)], writes=[PS(6)]))
                yield
                lag2[0]()
                lag2[0] = None
                yield
                P.op("act", lambda e: e.activation(out=m2, in_=psb[6][:, :], func=AF.Ln, scale=1.0 / 1024, bias=epsc[:, 0:1]), writes=[PS(6), "m2"])
                P.op("act", lambda e: e.activation(out=m2, in_=m2, func=AF.Exp, scale=-0.5), writes=["m2"])
                for c in range(8):
                    P.op("dve", lambda e, c=c: e.scalar_tensor_tensor(out=yT[:, c, t0:t0 + 512], in0=ycs[c], scalar=pv[:, 56 + c:57 + c], in1=m2, op0=ALU.mult, op1=ALU.mult), reads=[("yc", yb, c), "m2"], writes=[("yTc", c, qd)])
                    if c % 2 == 1:
                        yield

            def interleave(gm, gf, ratio=4):
                dm = df = False
                while not (dm and df):
                    if not dm:
                        try:
                            next(gm)
                        except StopIteration:
                            dm = True
                    for _ in range(ratio):
                        if not df:
                            try:
                                next(gf)
                            except StopIteration:
                                df = True

            load_w(0, 0)
            load_w(1, 1024)
            for _ in main_gen(0):
                pass
            for qd in range(1, 4):
                interleave(main_gen(qd), fin_gen(qd - 1))
            for _ in fin_gen(3):
                pass
            P.barrier()

        phase_conv()
        if stage == "B1":
            return dump(yT)

        def phase_attn():
            ar = Arena()
            o_w = [ar.bf(2048) for _ in range(6)]
            o_q = [ar.bf(2048) for _ in range(2)]
            o_k = [ar.bf(2048) for _ in range(2)]
            o_v = [ar.bf(2048) for _ in range(2)]
            o_stage = ar.f32(640)
            o_eb = ar.f32(640)
            o_q32 = [ar.f32(512) for _ in range(2)]
            o_sq = [ar.bf(512) for _ in range(2)]
            o_rs = ar.f32(512)
            o_ef = [ar.f32(640) for _ in range(3)]
            o_E = [ar.bf(640) for _ in range(3)]
            o_rz = ar.f32(128)
            o_o32 = ar.f32(128)
            o_osq = [ar.bf(128), ar.bf(128), ar.bf(128)]
            osqs = [R2b[:, o:o + 128] for o in o_osq]
            wsl = [R2b[:, o:o + 2048].rearrange("p (k c) -> p k c", c=128) for o in o_w]
            qTs = [R2b[:, o:o + 2048] for o in o_q]
            kTs = [R2b[:, o:o + 2048] for o in o_k]
            Vhs = [R2b[:, o:o + 2048].rearrange("p (t c) -> p t c", c=128) for o in o_v]
            stg = R2[:, o_stage:o_stage + 640]
            expB = R2[:, o_eb:o_eb + 640]
            q32 = [R2[:, o:o + 512] for o in o_q32]
            sq = [R2b[:, o:o + 512] for o in o_sq]
            rs = R2[:, o_rs:o_rs + 512]
            ef = [R2[:, o:o + 640] for o in o_ef]
            EB = [R2b[:, o:o + 640] for o in o_E]
            rz = R2[:, o_rz:o_rz + 128]
            o32 = R2[:, o_o32:o_o32 + 128]
            scale = 128.0 ** -0.5

            def load_w(slot, col0):
                for q in range(4):
                    P.dma("pool", f"aw{slot}", lambda e, q=q: e.dma_start(out=wsl[slot][:, 4 * q:4 * q + 4, :], in_=win_d.ap()[512 * q:512 * q + 512, col0:col0 + 128].rearrange("(k p) c -> p k c", p=128)), writes=[("aw", slot, q)])

            def load_head(h):
                s0 = 3 * (h % 2)
                load_w(s0, 2048 + h * 128)
                load_w(s0 + 1, 3072 + h * 128)
                load_w(s0 + 2, 4096 + h * 128)

            cnt = [0]

            def proj_gen(h):
                hp = h % 2
                s0 = 3 * hp
                qT, kT, Vh = qTs[hp], kTs[hp], Vhs[hp]
                pending = None
                for which, dst, gcol in ((0, qT, 72), (1, kT, 73)):
                    slot = s0 + which
                    for tb in range(4):
                        t0 = tb * 512
                        bq = cnt[0] % 2
                        cnt[0] += 1
                        for k in range(16):
                            P.op("pe", lambda e, k=k, bq=bq, slot=slot, t0=t0: e.matmul(psb[bq][:, :], lhsT=wsl[slot][:, k, :], rhs=hT[:, k, t0:t0 + 512], start=(k == 0), stop=(k == 15)), reads=[("aw", slot, q) for q in (3, 2, 1, 0)], writes=[PS(bq)])
                        P.op("act", lambda e, bq=bq: e.activation(out=q32[bq], in_=psb[bq][:, :], func=AF.Identity), writes=[PS(bq), ("q32", bq)])
                        P.op("act", lambda e, bq=bq: e.activation(out=sq[bq], in_=q32[bq], func=AF.Square), reads=[("q32", bq)], writes=[("sq", bq)])
                        if pending is not None:
                            pending()

                        def stage2(bq=bq, dst=dst, gcol=gcol, t0=t0, which=which):
                            P.op("pe", lambda e: e.matmul(psb[2][:, :], lhsT=onesb[:, :], rhs=sq[bq], start=True, stop=True), reads=[("sq", bq)], writes=[PS(2)])
                            P.op("act", lambda e: e.activation(out=rs, in_=psb[2][:, :], func=AF.Ln, scale=1.0 / 128, bias=epsc[:, 0:1]), writes=[PS(2), "rs"])
                            P.op("act", lambda e: e.activation(out=rs, in_=rs, func=AF.Exp, scale=-0.5), writes=["rs"])
                            P.op("dve", lambda e: e.scalar_tensor_tensor(out=dst[:, t0:t0 + 512], in0=q32[bq], scalar=pv[:, gcol:gcol + 1], in1=rs, op0=ALU.mult, op1=ALU.mult), reads=[("q32", bq), "rs"], writes=[("qk", hp, which)])
                        pending = stage2
                        yield
                for tg in range(4):
                    bV = cnt[0] % 2
                    cnt[0] += 1
                    pV = psb[bV][:, :].rearrange("p (j f) -> p j f", f=128)
                    for j in range(4):
                        tt = tg * 4 + j
                        for k in range(16):
                            P.op("pe", lambda e, k=k, j=j, tt=tt, pV=pV, s0=s0: e.matmul(pV[:, j, :], lhsT=hT[:, k, tt * 128:(tt + 1) * 128], rhs=wsl[s0 + 2][:, k, :], start=(k == 0), stop=(k == 15)), reads=[("aw", s0 + 2, q) for q in (3, 2, 1, 0)], writes=[PS(bV)])
                        if pending is not None:
                            pending()
                            pending = None
                        if j == 3:
                            P.op("act", lambda e, tg=tg, pV=pV, Vh=Vh, bV=bV: e.activation(out=Vh[:, tg * 4:(tg + 1) * 4, :], in_=pV, func=AF.Identity), writes=[PS(bV), ("Vh", hp)])
                        if j % 2 == 1:
                            yield

            def attn_gen(h):
                hp = h % 2
                qT, kT, Vh = qTs[hp], kTs[hp], Vhs[hp]
                P.dma("sp", "relb", lambda e, h=h: e.dma_start(out=stg, in_=relb_d.ap()[h]), writes=["stg"])
                P.op("act", lambda e: e.activation(out=expB, in_=stg, func=AF.Exp), reads=["stg"], writes=["expB"])
                P.op("dve", lambda e: e.memset(expB[64:128, 0:64], 0.0), writes=["expB"])
                P.op("dve", lambda e: e.memset(expB[0:64, 576:640], 0.0), writes=["expB"])

                def scores(t):
                    nj = min(t, 4) + 1
                    wA = min(nj, 4) * 128
                    bA = 4 + (t % 2)
                    ei = t % 3
                    c6 = 128 * (t % 2)
                    for j in range(nj):
                        m = t - j
                        if j < 4:
                            P.op("pe", lambda e, j=j, m=m: e.matmul(psb[bA][:, j * 128:(j + 1) * 128], lhsT=kT[:, m * 128:(m + 1) * 128], rhs=qT[:, t * 128:(t + 1) * 128], start=True, stop=True), reads=[("qk", hp, 0), ("qk", hp, 1)], writes=[PS(bA)])
                        else:
                            P.op("pe", lambda e, m=m: e.matmul(psb[6][:, c6:c6 + 128], lhsT=kT[:, m * 128:(m + 1) * 128], rhs=qT[:, t * 128:(t + 1) * 128], start=True, stop=True), reads=[("qk", hp, 0), ("qk", hp, 1)], writes=[PS(6)])
                    P.op("act", lambda e: e.activation(out=ef[ei][:, 0:wA], in_=psb[bA][:, 0:wA], func=AF.Exp, scale=scale), writes=[PS(bA), ("ef", ei)])
                    if nj == 5:
                        P.op("act", lambda e: e.activation(out=ef[ei][:, 512:640], in_=psb[6][:, c6:c6 + 128], func=AF.Exp, scale=scale), writes=[PS(6), ("ef", ei)])
                    wE = nj * 128
                    P.op("dve", lambda e: e.tensor_tensor(out=EB[ei][:, 0:wE], in0=ef[ei][:, 0:wE], in1=expB[:, 0:wE], op=ALU.mult), reads=[("ef", ei), "expB"], writes=[("EB", ei)])

                def pvstep(t):
                    nj = min(t, 4) + 1
                    ei = t % 3
                    b7 = 7 if t % 2 else 3
                    for j in range(nj):
                        m = t - j
                        P.op("pe", lambda e, j=j, m=m: e.matmul(psb[b7][:, 0:128], lhsT=Vh[:, m, :], rhs=EB[ei][:, j * 128:(j + 1) * 128], start=(j == 0), stop=(j == nj - 1)), reads=[("EB", ei), ("Vh", hp)], writes=[PS(b7)])
                    for j in range(nj):
                        P.op("pe", lambda e, j=j: e.matmul(psb[b7][:, 128:256], lhsT=onesb[:, :], rhs=EB[ei][:, j * 128:(j + 1) * 128], start=(j == 0), stop=(j == nj - 1)), reads=[("EB", ei)], writes=[PS(b7)])
                    if t >= 2:
                        ssa_mm(t - 2, b7)
                    P.op("dve", lambda e: e.reciprocal(out=rz, in_=psb[b7][:, 128:256]), writes=[PS(b7), "rz"])
                    P.op("dve", lambda e: e.tensor_tensor(out=o32, in0=psb[b7][:, 0:128], in1=rz, op=ALU.mult), reads=["rz"], writes=[PS(b7), "o32"])
                    if t >= 2:
                        ssa_add(t - 2, b7)
                    oi = t % 3
                    P.op("act", lambda e: e.activation(out=osqs[oi], in_=o32, func=AF.Square), reads=["o32"], writes=[("osq", oi)])
                    P.op("act", lambda e: e.activation(out=yT[:, 8 + h, t * 128:(t + 1) * 128], in_=o32, func=AF.Copy, scale=pv[:, 64 + h:65 + h]), reads=["o32"], writes=[("yTa", h, t)])

                def ssa_mm(t, b7):
                    oi = t % 3
                    P.op("pe", lambda e: e.matmul(psb[b7][:, 256:257], lhsT=osqs[oi], rhs=onesb[:, 0:1], start=True, stop=True), reads=[("osq", oi)], writes=[PS(b7)])

                def ssa_add(t, b7):
                    P.op("dve", lambda e: e.tensor_tensor(out=ssa[:, t:t + 1], in0=ssa[:, t:t + 1], in1=psb[b7][:, 256:257], op=ALU.add), writes=[PS(b7), ("ssa", t)])

                scores(0)
                yield
                for t in range(NT):
                    if t + 1 < NT:
                        scores(t + 1)
                    pvstep(t)
                    yield
                for t in (NT - 2, NT - 1):
                    ssa_mm(t, 7)
                    ssa_add(t, 7)
                yield

            def drain(g):
                for _ in g:
                    pass

            load_head(0)
            load_head(1)
            drain(proj_gen(0))
            for h in range(8):
                ga = attn_gen(h)
                gp = proj_gen(h + 1) if h + 1 < 8 else None
                if h + 2 < 8:
                    load_head(h + 2)
                da = dp = False
                while not (da and (dp or gp is None)):
                    if gp is not None and not dp:
                        try:
                            next(gp)
                        except StopIteration:
                            dp = True
                    if not da:
                        try:
                            next(ga)
                        except StopIteration:
                            da = True
            P.barrier()

        phase_attn()
        P.op("act", lambda e: e.activation(out=tcol[:, :], in_=ssa[:, :], func=AF.Ln, scale=1.0 / 1024, bias=epsc[:, 0:1]), writes=["tcol"])
        P.op("act", lambda e: e.activation(out=ra[:, :], in_=tcol[:, :], func=AF.Exp, scale=-0.5), reads=["tcol"], writes=["ra"])
        P.barrier()
        if stage == "B2":
            return dump(yT, ra)

        def phase_out():
            ar = Arena()
            o_w = [ar.bf(8192) for _ in range(2)]
            o_x = [ar.f32(512) for _ in range(3)]
            o_y = [ar.f32(512) for _ in range(3)]
            wsl = [R2b[:, o:o + 8192].rearrange("p (k c) -> p k c", c=512) for o in o_w]
            xs = [R2[:, o:o + 512] for o in o_x]
            ys = [R2[:, o:o + 512] for o in o_y]

            def load_w(cb):
                s = cb % 2
                for q in range(4):
                    P.dma("pool", f"ow{s}", lambda e, q=q: e.dma_start(out=wsl[s][:, 4 * q:4 * q + 4, :], in_=wout_d.ap()[512 * q:512 * q + 512, cb * 512:(cb + 1) * 512].rearrange("(k p) c -> p k c", p=128)), writes=[("ow", s, q)])

            load_w(0)
            n = 0
            tiles = [(cb, tt) for cb in range(4) for tt in range(NT)]

            def load_x(i):
                if i >= len(tiles):
                    return
                cb, tt = tiles[i]
                xi = i % 3
                P.dma("sp", f"ox{xi}", lambda e: e.dma_start(out=xs[xi], in_=x_d.ap()[tt * 128:(tt + 1) * 128, cb * 512:(cb + 1) * 512]), writes=[("ox", xi)])

            load_x(0)
            load_x(1)
            for cb in range(4):
                if cb + 1 < 4:
                    load_w(cb + 1)
                s = cb % 2
                for tt in range(NT):
                    xi = n % 3
                    bc = n % 2
                    ba = 2 + n % 2
                    load_x(n + 2)
                    n += 1
                    for k in range(8):
                        P.op("pe", lambda e, k=k, tt=tt, s=s, bc=bc: e.matmul(psb[bc][:, :], lhsT=yT[:, k, tt * 128:(tt + 1) * 128], rhs=wsl[s][:, k, :], start=(k == 0), stop=(k == 7)), reads=[("ow", s, q) for q in (3, 2, 1, 0)], writes=[PS(bc)])
                    for k in range(8, 16):
                        P.op("pe", lambda e, k=k, tt=tt, s=s, ba=ba: e.matmul(psb[ba][:, :], lhsT=yT[:, k, tt * 128:(tt + 1) * 128], rhs=wsl[s][:, k, :], start=(k == 8), stop=(k == 15)), reads=[("ow", s, q) for q in (3, 2, 1, 0)], writes=[PS(ba)])
                    P.op("dve", lambda e, bc=bc, xi=xi: e.tensor_tensor(out=ys[xi], in0=psb[bc][:, :], in1=xs[xi], op=ALU.add), reads=[("ox", xi)], writes=[PS(bc), ("oy", xi)])
                    P.op("dve", lambda e, ba=ba, xi=xi, tt=tt: e.scalar_tensor_tensor(out=ys[xi], in0=psb[ba][:, :], scalar=ra[:, tt:tt + 1], in1=ys[xi], op0=ALU.mult, op1=ALU.add), writes=[PS(ba), ("oy", xi)])
                    P.dma("sp", f"oy{xi}", lambda e, tt=tt, cb=cb, xi=xi: e.dma_start(out=out_d.ap()[tt * 128:(tt + 1) * 128, cb * 512:(cb + 1) * 512], in_=ys[xi]), reads=[("oy", xi)])
            P.barrier()

        phase_out()
        if stage == "C":
            return dump(yT, ra)

        phase_norm_T(out_d, 16, hT, "A2")
        if stage == "A2":
            return dump(hT, rcol)

        def phase_ffn():
            ar = Arena()
            o_f = ar.bf(8 * 1024)
            o_w1 = [ar.bf(16 * 256) for _ in range(3)]
            o_w2 = [ar.bf(8 * 512) for _ in range(3)]
            o_r = [ar.f32(512) for _ in range(2)]
            fT = R2b[:, o_f:o_f + 8192].rearrange("p (j t) -> p j t", t=1024)
            w1s = [R2b[:, o:o + 4096].rearrange("p (k c) -> p k c", c=256) for o in o_w1]
            w2s = [R2b[:, o:o + 4096].rearrange("p (j c) -> p j c", c=512) for o in o_w2]
            rl = [R2[:, o:o + 512] for o in o_r]
            acc = R1f
            n1 = [0]
            n2 = [0]
            w1q = []
            w2q = []
            for hf in range(2):
                for g in range(8):
                    for jj in range(4):
                        w1q.append((hf, g, jj))
                    for cb in range(4):
                        w2q.append((hf, g, cb))
            w1i = [0]
            w2i = [0]

            def issue_w1():
                if w1i[0] >= len(w1q):
                    return
                hf, g, jj = w1q[w1i[0]]
                s = w1i[0] % 3
                w1i[0] += 1
                c0 = g * 1024 + jj * 256
                for q in range(4):
                    P.dma("pool", f"w1{s}", lambda e, q=q: e.dma_start(out=w1s[s][:, 4 * q:4 * q + 4, :], in_=w1_d.ap()[512 * q:512 * q + 512, c0:c0 + 256].rearrange("(k p) c -> p k c", p=128)), writes=[("w1", s, q)])

            def issue_w2():
                if w2i[0] >= len(w2q):
                    return
                hf, g, cb = w2q[w2i[0]]
                s = w2i[0] % 3
                w2i[0] += 1
                for q in range(2):
                    P.dma("pool", f"w2{s}", lambda e, q=q: e.dma_start(out=w2s[s][:, 4 * q:4 * q + 4, :], in_=w2_d.ap()[g * 1024 + 512 * q:g * 1024 + 512 * q + 512, cb * 512:(cb + 1) * 512].rearrange("(j p) c -> p j c", p=128)), writes=[("w2", s, q)])

            issue_w1()
            issue_w1()
            issue_w2()
            issue_w2()
            u1 = 0
            u2 = 0
            for hf in range(2):
                for q in range(4):
                    P.dma("sp", "acc", lambda e, hf=hf, q=q: e.dma_start(out=acc[:, 2 * q:2 * q + 2, :], in_=out_d.ap()[hf * 1024 + 256 * q:hf * 1024 + 256 * q + 256, :].rearrange("(t p) c -> p t c", p=128)), writes=[("accl", q), ("accw", q)])
                for g in range(8):
                    for jj in range(4):
                        issue_w1()
                        s = u1 % 3
                        u1 += 1
                        for j2 in range(2):
                            j = jj * 2 + j2
                            for tb in range(2):
                                t0 = hf * 1024 + tb * 512
                                b = n1[0] % 4
                                ri = n1[0] % 2
                                n1[0] += 1
                                for k in range(16):
                                    P.op("pe", lambda e, k=k, b=b, s=s, j2=j2, t0=t0: e.matmul(psb[b][:, :], lhsT=w1s[s][:, k, j2 * 128:(j2 + 1) * 128], rhs=hT[:, k, t0:t0 + 512], start=(k == 0), stop=(k == 15)), reads=[("w1", s, q) for q in (3, 2, 1, 0)], writes=[PS(b)])
                                P.op("act", lambda e, b=b, ri=ri: e.activation(out=rl[ri], in_=psb[b][:, :], func=AF.Relu), writes=[PS(b), ("rl", ri)])
                                P.op("act", lambda e, ri=ri, j=j, tb=tb: e.activation(out=fT[:, j, tb * 512:(tb + 1) * 512], in_=rl[ri], func=AF.Square), reads=[("rl", ri)], writes=[("fT", j)])
                    for cb in range(4):
                        issue_w2()
                        s = u2 % 3
                        u2 += 1
                        for tt in range(8):
                            b = 4 + n2[0] % 4
                            n2[0] += 1
                            for j in range(8):
                                P.op("pe", lambda e, j=j, b=b, s=s, tt=tt: e.matmul(psb[b][:, :], lhsT=fT[:, j, tt * 128:(tt + 1) * 128], rhs=w2s[s][:, j, :], start=(j == 0), stop=(j == 7)), reads=[("w2", s, 1), ("w2", s, 0), ("fT", j)], writes=[PS(b)])
                            P.op("dve", lambda e, b=b, tt=tt, cb=cb: e.tensor_tensor(out=acc[:, tt, cb * 512:(cb + 1) * 512], in0=acc[:, tt, cb * 512:(cb + 1) * 512], in1=psb[b][:, :], op=ALU.add), reads=[("accl", tt // 2)], writes=[PS(b), ("acc_t", tt, cb)])
                for q in range(4):
                    P.dma("sp", "accout", lambda e, hf=hf, q=q: e.dma_start(out=out_d.ap()[hf * 1024 + 256 * q:hf * 1024 + 256 * q + 256, :].rearrange("(t p) c -> p t c", p=128), in_=acc[:, 2 * q:2 * q + 2, :]), reads=[("acc_t", tt, cb) for tt in range(8) for cb in range(4)], writes=[("accw", q)])
                if hf == 1:
                    P.barrier()

        phase_ffn()
        P.wait_all_dma("sp")
        P.emit(nc)
    return nc


def _prep_shared(inputs):
    f = lambda k: np.ascontiguousarray(np.asarray(inputs[k], dtype=np.float32)[0])
    pvec = np.zeros((128, NPV), np.float32)
    pvec[:, 0:16] = f("ln1_g").reshape(16, 128).T
    pvec[:, 16:32] = f("ln2_g").reshape(16, 128).T
    pvec[:, 32:40] = f("b_dw").reshape(8, 128).T
    pvec[:, 40:48] = f("conv_ln_g").reshape(8, 128).T
    pvec[:, 48:56] = f("conv_ln_b").reshape(8, 128).T
    pvec[:, 56:64] = f("out_norm_conv_g").reshape(8, 128).T
    pvec[:, 64:72] = f("out_norm_attn_g").reshape(8, 128).T
    pvec[:, 72] = f("q_norm_g")
    pvec[:, 73] = f("k_norm_g")
    wdw = f("w_dw")
    pvec[:, 74:] = wdw.reshape(31, 8, 128).transpose(2, 1, 0).reshape(128, 248)
    rel = np.arange(640)[None, :] - np.arange(128)[:, None]
    idx = np.clip(rel, -128, 128) + 128
    relb = np.ascontiguousarray(f("rel_bias")[:, idx])
    return {
        "w_in": f("w_in"), "w_out": f("w_out"), "w_ff1": f("w_ff1"), "w_ff2": f("w_ff2"),
        "pvec": pvec, "relb": relb, "ident": np.eye(128, dtype=np.float32),
    }


def kernel(**inputs):
    x = np.asarray(inputs["x"], dtype=np.float32)
    shared = _prep_shared(inputs)
    nc = build_program("full")
    in_maps = [dict(shared, x=np.ascontiguousarray(x[b])) for b in range(8)]
    res = run_bass_kernel_spmd(nc, in_maps, core_ids=list(range(8)))
    return np.stack([np.asarray(r["out"], dtype=np.float32) for r in res.results], axis=0)
```

```python
from contextlib import ExitStack
import numpy as np
import concourse.bass as bass
import concourse.mybir as mybir
from concourse.bass_utils import run_bass_kernel_spmd

F32 = mybir.dt.float32
BF16 = mybir.dt.bfloat16
AF = mybir.ActivationFunctionType
ALU = mybir.AluOpType

S = 2048
D = 2048
NT = 16
DFF = 8192
EPS = 1e-6
NPV = 74 + 8 * 31

ENGS = ("pe", "act", "dve", "pool", "sp")
SEM_EPOCH = 30000


class Op:
    __slots__ = ("eng", "fn", "idx", "signal", "waits", "dma_sem", "dma_val", "is_dma")

    def __init__(self, eng, fn, idx):
        self.eng = eng
        self.fn = fn
        self.idx = idx
        self.signal = False
        self.waits = []
        self.dma_sem = None
        self.dma_val = 0
        self.is_dma = False


class Prog:
    def __init__(self, same_engine_sync=True):
        self.ops = {e: [] for e in ENGS}
        self.last_w = {}
        self.readers = {}
        self.waited = {e: {p: -1 for p in ENGS} for e in ENGS}
        self.dma_waited = {e: {} for e in ENGS}
        self.dma_sems = {}
        self.same_engine_sync = same_engine_sync
        self.all_dma = []

    def _add_wait(self, op, dep):
        if dep is None or dep is op:
            return
        e = op.eng
        if dep.is_dma:
            cur = self.dma_waited[e].get(dep.dma_sem, 0)
            if cur >= dep.dma_val:
                return
            self.dma_waited[e][dep.dma_sem] = dep.dma_val
            op.waits.append(dep)
            return
        if dep.eng == e:
            if e == "pe" or not self.same_engine_sync:
                return
        if self.waited[e][dep.eng] >= dep.idx:
            return
        self.waited[e][dep.eng] = dep.idx
        dep.signal = True
        op.waits.append(dep)

    def _deps(self, op, reads, writes):
        for r in reads:
            self._add_wait(op, self.last_w.get(r))
        for w in writes:
            self._add_wait(op, self.last_w.get(w))
            for rd in self.readers.get(w, ()):
                self._add_wait(op, rd)
        for r in reads:
            self.readers.setdefault(r, []).append(op)
        for w in writes:
            self.last_w[w] = op
            self.readers[w] = []

    def op(self, eng, fn, reads=(), writes=()):
        o = Op(eng, fn, len(self.ops[eng]))
        self._deps(o, reads, writes)
        self.ops[eng].append(o)
        return o

    def dma(self, eng, semkey, fn, reads=(), writes=()):
        o = Op(eng, fn, len(self.ops[eng]))
        o.is_dma = True
        c = self.dma_sems.setdefault(semkey, [0])
        c[0] += 16
        o.dma_sem = semkey
        o.dma_val = c[0]
        self._deps(o, reads, writes)
        self.ops[eng].append(o)
        self.all_dma.append(o)
        return o

    def barrier(self):
        lastc = {}
        for e in ENGS:
            lastc[e] = None
            for o in reversed(self.ops[e]):
                if not o.is_dma and o.fn is not None:
                    lastc[e] = o
                    break
        latest_dma = {}
        for o in self.all_dma:
            latest_dma[o.dma_sem] = o
        for e in ENGS:
            o = Op(e, None, len(self.ops[e]))
            for p in ENGS:
                if p != e:
                    self._add_wait(o, lastc[p])
            for d in latest_dma.values():
                self._add_wait(o, d)
            self.ops[e].append(o)
        self.last_w = {}
        self.readers = {}

    def wait_all_dma(self, eng):
        latest = {}
        for o in self.all_dma:
            latest[o.dma_sem] = o
        o = Op(eng, None, len(self.ops[eng]))
        for d in latest.values():
            self._add_wait(o, d)
        self.ops[eng].append(o)

    def emit(self, nc):
        with ExitStack() as st:
            esems = {}
            for e in ENGS:
                nsig = sum(1 for o in self.ops[e] if o.signal and not o.is_dma)
                n = max(1, (nsig + SEM_EPOCH - 1) // SEM_EPOCH)
                esems[e] = [st.enter_context(nc.semaphore(f"s_{e}_{i}")) for i in range(n)]
            dsems = {}
            for i, k in enumerate(self.dma_sems):
                dsems[k] = st.enter_context(nc.semaphore(f"d_{i}"))
            sig = {}
            for e in ENGS:
                c = 0
                for o in self.ops[e]:
                    if o.signal and not o.is_dma:
                        sig[o] = (esems[e][c // SEM_EPOCH], c % SEM_EPOCH + 1)
                        c += 1
            block = st.enter_context(nc.Block())

            def run(e, eng):
                for o in self.ops[e]:
                    for d in o.waits:
                        if d.is_dma:
                            eng.wait_ge(dsems[d.dma_sem], d.dma_val)
                        else:
                            s, v = sig[d]
                            eng.wait_ge(s, v)
                    if o.fn is None:
                        continue
                    ins = o.fn(eng)
                    if o.is_dma:
                        ins.then_inc(dsems[o.dma_sem], 16)
                    elif o.signal:
                        s, v = sig[o]
                        ins.then_inc(s, 1)

            @block.tensor
            def _(eng):
                run("pe", eng)

            @block.scalar
            def _(eng):
                run("act", eng)

            @block.vector
            def _(eng):
                run("dve", eng)

            @block.gpsimd
            def _(eng):
                run("pool", eng)

            @block.sync
            def _(eng):
                run("sp", eng)


def build_program(stage="full", same_engine_sync=True):
    nc = bass.Bass("TRN2", target_bir_lowering=False)
    x_d = nc.dram_tensor("x", [S, D], F32, kind="ExternalInput")
    win_d = nc.dram_tensor("w_in", [D, 5120], F32, kind="ExternalInput")
    wout_d = nc.dram_tensor("w_out", [D, D], F32, kind="ExternalInput")
    w1_d = nc.dram_tensor("w_ff1", [D, DFF], F32, kind="ExternalInput")
    w2_d = nc.dram_tensor("w_ff2", [DFF, D], F32, kind="ExternalInput")
    pv_d = nc.dram_tensor("pvec", [128, NPV], F32, kind="ExternalInput")
    relb_d = nc.dram_tensor("relb", [8, 128, 640], F32, kind="ExternalInput")
    id_d = nc.dram_tensor("ident", [128, 128], F32, kind="ExternalInput")
    out_d = nc.dram_tensor("out", [S, D], F32, kind="ExternalOutput")
    dbg_d = None
    if stage != "full":
        dbg_d = nc.dram_tensor("dbg", [128, 16, 2048], BF16, kind="ExternalOutput")
        dbg2_d = nc.dram_tensor("dbg2", [128, 16], F32, kind="ExternalOutput")

    P = Prog(same_engine_sync=same_engine_sync)
    with ExitStack() as st:
        def sb(name, shape, dt):
            return st.enter_context(nc.sbuf_tensor(name, shape, dt))

        R0 = sb("R0", [128, 16, 2048], BF16)
        R1 = sb("R1", [128, 16, 2048], BF16)
        R2 = sb("R2", [128, 18944], F32)
        pv = sb("pv", [128, NPV], F32)
        ident = sb("identf", [128, 128], F32)
        identb = sb("identb", [128, 128], BF16)
        onesf = sb("onesf", [128, 128], F32)
        onesb = sb("onesb", [128, 128], BF16)
        ssa = sb("ssa", [128, 16], F32)
        ra = sb("ra", [128, 16], F32)
        sscol = sb("sscol", [128, 16], F32)
        rcol = sb("rcol", [128, 16], F32)
        tcol = sb("tcol", [128, 16], F32)
        halo = sb("halo", [128, 8, 32], BF16)
        epsc = sb("epsc", [128, 1], F32)
        psb = [st.enter_context(nc.psum_tensor(f"ps{i}", [128, 512], F32)) for i in range(8)]

        R2b = R2.bitcast(BF16)
        R1f = R1.bitcast(F32).reshape([128, 8, 2048])

        hT = R0
        yT = R1

        class Arena:
            def __init__(self):
                self.off = 0

            def f32(self, n):
                o = self.off
                self.off += n
                assert self.off <= 18944, self.off
                return o

            def bf(self, n):
                o = self.off
                self.off += (n + 1) // 2
                assert self.off <= 18944, self.off
                return 2 * o

        def PS(b):
            return ("ps", b)

        P.dma("sp", "pv", lambda e: e.dma_start(out=pv[:, :], in_=pv_d.ap()), writes=["pv"])
        P.dma("sp", "id", lambda e: e.dma_start(out=ident[:, :], in_=id_d.ap()), writes=["ident"])
        P.op("dve", lambda e: e.tensor_copy(out=identb[:, :], in_=ident[:, :]), reads=["ident"], writes=["identb"])
        P.op("pool", lambda e: e.memset(onesf[:, :], 1.0), writes=["onesf"])
        P.op("pool", lambda e: e.memset(onesb[:, :], 1.0), writes=["onesb"])
        P.op("pool", lambda e: e.memset(ssa[:, :], 0.0), writes=["ssa"])
        P.op("pool", lambda e: e.memset(epsc[:, :], EPS), writes=["epsc"])
        P.op("pool", lambda e: e.memset(sscol[:, :], 0.0), writes=["sscol"])
        P.barrier()

        def phase_norm_T(src_d, gcol0, dstT, tag):
            ar = Arena()
            NXB = 4
            o_x = [ar.f32(2048) for _ in range(NXB)]
            o_g = ar.f32(2048)
            o_j = ar.bf(2048)
            xt = [R2[:, o_x[i]:o_x[i] + 2048] for i in range(NXB)]
            Gb = R2[:, o_g:o_g + 2048].rearrange("p (c f) -> p c f", f=128)
            junk = R2b[:, o_j:o_j + 2048]
            o_dg = [ar.f32(128) for _ in range(NXB)]
            dg = [R2[:, o:o + 128] for o in o_dg]
            for c in range(16):
                P.op("dve", lambda e, c=c: e.tensor_scalar(out=Gb[:, c, :], in0=onesf[:, :], scalar1=pv[:, gcol0 + c:gcol0 + c + 1], scalar2=None, op0=ALU.mult), writes=[("Gb", c)])
            P.op("dve", lambda e: e.memset(sscol[:, :], 0.0), writes=["sscol"])
            def load(tt):
                if tt >= NT:
                    return
                i = tt % NXB
                P.dma("sp", f"xt{i}", lambda e: e.dma_start(out=xt[i], in_=src_d.ap()[tt * 128:(tt + 1) * 128, :]), writes=[("xt", i)])

            def stage1(tt):
                i = tt % NXB
                P.op("act", lambda e: e.activation(out=junk, in_=xt[i], func=AF.Square, accum_out=sscol[:, tt:tt + 1]), reads=[("xt", i), "sscol"], writes=["junk", ("ss", tt)])
                P.op("act", lambda e: e.activation(out=tcol[:, tt:tt + 1], in_=sscol[:, tt:tt + 1], func=AF.Ln, scale=1.0 / D, bias=epsc[:, 0:1]), reads=[("ss", tt)], writes=[("tc", tt)])
                P.op("act", lambda e: e.activation(out=rcol[:, tt:tt + 1], in_=tcol[:, tt:tt + 1], func=AF.Exp, scale=-0.5), reads=[("tc", tt)], writes=[("rc", tt)])
                P.op("dve", lambda e: e.tensor_scalar(out=dg[i], in0=ident[:, :], scalar1=rcol[:, tt:tt + 1], scalar2=None, op0=ALU.mult), reads=[("rc", tt)], writes=[("dg", i)])

            def stage2(tt):
                i = tt % NXB
                b0 = 4 * (tt % 2)
                for g in range(4):
                    pT = psb[b0 + g][:, :].rearrange("p (j f) -> p j f", f=128)
                    for j in range(4):
                        c = g * 4 + j
                        P.op("pe", lambda e, pT=pT, j=j, c=c: e.matmul(pT[:, j, :], lhsT=xt[i][:, c * 128:(c + 1) * 128], rhs=dg[i], start=True, stop=True), reads=[("xt", i), ("dg", i)], writes=[PS(b0 + g)])
                    P.op("dve", lambda e, pT=pT, g=g: e.tensor_tensor(out=dstT[:, g * 4:(g + 1) * 4, tt * 128:(tt + 1) * 128], in0=pT, in1=Gb[:, g * 4:(g + 1) * 4, :], op=ALU.mult), reads=[("Gb", c) for c in range(g * 4, g * 4 + 4)], writes=[PS(b0 + g)])

            for tt in range(NXB - 1):
                load(tt)
            stage1(0)
            for tt in range(NT):
                load(tt + NXB - 1)
                if tt + 1 < NT:
                    stage1(tt + 1)
                stage2(tt)
            P.barrier()

        phase_norm_T(x_d, 0, hT, "A")

        def dump(t3, small=None):
            P.barrier()
            P.dma("sp", "dbg", lambda e: e.dma_start(out=dbg_d.ap(), in_=t3[:, :, :]))
            if small is not None:
                P.dma("sp", "dbg2", lambda e: e.dma_start(out=dbg2_d.ap(), in_=small[:, :]))
            P.wait_all_dma("sp")
            P.emit(nc)
            return nc

        if stage == "A":
            return dump(hT, rcol)

        def phase_conv():
            ar = Arena()
            o_w = [ar.bf(4096) for _ in range(4)]
            o_G = [ar.bf(544) for _ in range(2)]
            o_D = [ar.bf(31 * 128) for _ in range(2)]
            o_sig = [ar.f32(512) for _ in range(2)]
            o_tmp = [ar.bf(512) for _ in range(2)]
            o_t1 = [ar.f32(512) for _ in range(2)]
            o_mu = ar.f32(512)
            o_rs = ar.f32(512)
            o_m2 = ar.f32(512)
            wsl = [R2b[:, o:o + 4096].rearrange("p (k c) -> p k c", c=256) for o in o_w]
            Gbuf = [R2b[:, o:o + 544] for o in o_G]
            Dks = [R2b[:, o:o + 31 * 128].rearrange("p (k c) -> p k c", c=128) for o in o_D]
            sig = [R2[:, o:o + 512] for o in o_sig]
            tmp = [R2b[:, o:o + 512] for o in o_tmp]
            t1 = [R2[:, o:o + 512] for o in o_t1]
            mu = R2[:, o_mu:o_mu + 512]
            rs = R2[:, o_rs:o_rs + 512]
            m2 = R2[:, o_m2:o_m2 + 512]
            ycq = [R1f[:, 4 + 2 * i:6 + 2 * i, :].rearrange("p a (b f) -> p (a b) f", f=512) for i in range(2)]
            cnt = [0]

            def load_w(slot, col0):
                for q in range(4):
                    P.dma("pool", f"cw{slot}", lambda e, q=q: e.dma_start(out=wsl[slot][:, 4 * q:4 * q + 4, :], in_=win_d.ap()[512 * q:512 * q + 512, col0:col0 + 256].rearrange("(k p) c -> p k c", p=128)), writes=[("cw", slot, q)])

            seq = [(qd, cc) for qd in range(4) for cc in range(8)]

            pend = [None]

            def main_gen(qd):
                yc = ycq[qd % 2]
                t0 = qd * 512
                for cc in range(8):
                    si = qd * 8 + cc
                    pi = si // 2
                    sv = (2 * pi) % 4
                    sg = sv + 1
                    wc = (cc % 2) * 128
                    if cc % 2 == 0 and pi + 1 < len(seq) // 2:
                        ncc = seq[2 * (pi + 1)][1]
                        load_w((2 * pi + 2) % 4, ncc * 128)
                        load_w((2 * pi + 3) % 4, 1024 + ncc * 128)
                    gi = si % 2
                    G = Gbuf[gi]
                    Dk = Dks[gi]
                    bv, bg, bc = si % 2, 2 + si % 2, 4 + si % 2
                    P.op("dve", lambda e, cc=cc, Dk=Dk: e.tensor_tensor(out=Dk, in0=bass.AP(identb, 0, [[128, 128], [0, 31], [1, 128]]), in1=bass.AP(pv, 74 + cc * 31, [[NPV, 128], [1, 31], [0, 128]]), op=ALU.mult), writes=[("Dk", gi)])
                    if qd == 0:
                        P.op("dve", lambda e, G=G: e.memset(G[:, 0:30], 0.0), writes=[("G", gi)])
                    else:
                        P.op("dve", lambda e, G=G, cc=cc: e.tensor_copy(out=G[:, 0:30], in_=halo[:, cc, 0:30]), reads=[("halo", cc)], writes=[("G", gi)])
                    for k in range(16):
                        P.op("pe", lambda e, k=k, bv=bv, sv=sv, wc=wc: e.matmul(psb[bv][:, :], lhsT=wsl[sv][:, k, wc:wc + 128], rhs=hT[:, k, t0:t0 + 512], start=(k == 0), stop=(k == 15)), reads=[("cw", sv, q) for q in (3, 2, 1, 0)], writes=[PS(bv)])
                    for k in range(16):
                        P.op("pe", lambda e, k=k, bg=bg, sg=sg, wc=wc: e.matmul(psb[bg][:, :], lhsT=wsl[sg][:, k, wc:wc + 128], rhs=hT[:, k, t0:t0 + 512], start=(k == 0), stop=(k == 15)), reads=[("cw", sg, q) for q in (3, 2, 1, 0)], writes=[PS(bg)])
                    P.op("act", lambda e, bg=bg, gi=gi: e.activation(out=sig[gi], in_=psb[bg][:, :], func=AF.Sigmoid), writes=[PS(bg), ("sig", gi)])
                    P.op("dve", lambda e, bv=bv, gi=gi, G=G: e.tensor_tensor(out=G[:, 30:542], in0=psb[bv][:, :], in1=sig[gi], op=ALU.mult), reads=[("sig", gi)], writes=[PS(bv), ("G", gi)])
                    if qd < 3:
                        P.op("dve", lambda e, G=G, cc=cc: e.tensor_copy(out=halo[:, cc, 0:30], in_=G[:, 512:542]), reads=[("G", gi)], writes=[("halo", cc)])
                    if pend[0] is not None:
                        pend[0]()
                    yield

                    def conv(bc=bc, G=G, Dk=Dk, gi=gi, cc=cc):
                        for k in range(31):
                            P.op("pe", lambda e, k=k: e.matmul(psb[bc][:, :], lhsT=Dk[:, k, :], rhs=G[:, k:k + 512], start=(k == 0), stop=(k == 30)), reads=[("Dk", gi), ("G", gi)], writes=[PS(bc)])
                        P.op("act", lambda e: e.activation(out=yc[:, cc, :], in_=psb[bc][:, :], func=AF.Identity, bias=pv[:, 32 + cc:33 + cc]), writes=[PS(bc), ("yc", qd % 2, cc)])
                    pend[0] = conv
                pend[0]()
                pend[0] = None
                yield

            def fin_gen(qd):
                yc = ycq[qd % 2]
                yb = qd % 2
                t0 = qd * 512
                ycs = [yc[:, c, :] for c in range(8)]
                for c in range(8):
                    P.op("pe", lambda e, c=c: e.matmul(psb[6][:, :], lhsT=onesf[:, :], rhs=ycs[c], start=(c == 0), stop=(c == 7)), reads=[("yc", yb, c)], writes=[PS(6)])
                    if c % 2 == 1:
                        yield
                lag = None
                for c in range(8):
                    ti = cnt[0] % 2
                    cnt[0] += 1
                    P.op("act", lambda e, c=c, ti=ti: e.activation(out=tmp[ti], in_=ycs[c], func=AF.Square), reads=[("yc", yb, c)], writes=[("tmp", ti)])
                    yield
                    if lag is not None:
                        lag()
                    lag = (lambda c=c, ti=ti: P.op("pe", lambda e: e.matmul(psb[7][:, :], lhsT=onesb[:, :], rhs=tmp[ti], start=(c == 0), stop=(c == 7)), reads=[("tmp", ti)], writes=[PS(7)]))
                yield
                lag()
                P.op("act", lambda e: e.activation(out=mu, in_=psb[6][:, :], func=AF.Identity, scale=1.0 / 1024), writes=[PS(6), "mu"])
                P.op("dve", lambda e: e.tensor_tensor(out=m2, in0=mu, in1=mu, op=ALU.mult), reads=["mu"], writes=["m2"])
                P.op("dve", lambda e: e.scalar_tensor_tensor(out=rs, in0=psb[7][:, :], scalar=1.0 / 1024, in1=m2, op0=ALU.mult, op1=ALU.subtract), reads=["m2"], writes=[PS(7), "rs"])
                P.op("act", lambda e: e.activation(out=rs, in_=rs, func=AF.Ln, bias=epsc[:, 0:1]), writes=["rs"])
                P.op("act", lambda e: e.activation(out=rs, in_=rs, func=AF.Exp, scale=-0.5), writes=["rs"])
                yield
                lag2 = [None]
                for c in range(8):
                    ti = c % 2
                    P.op("dve", lambda e, c=c, ti=ti: e.tensor_tensor(out=t1[ti], in0=ycs[c], in1=mu, op=ALU.subtract), reads=[("yc", yb, c), "mu"], writes=[("t1", ti)])
                    P.op("dve", lambda e, ti=ti: e.tensor_tensor(out=t1[ti], in0=t1[ti], in1=rs, op=ALU.mult), reads=["rs"], writes=[("t1", ti)])
                    P.op("act", lambda e, c=c, ti=ti: e.activation(out=ycs[c], in_=t1[ti], func=AF.Silu, scale=pv[:, 40 + c:41 + c], bias=pv[:, 48 + c:49 + c]), reads=[("t1", ti)], writes=[("yc", yb, c)])
                    tj = cnt[0] % 2
                    cnt[0] += 1
                    P.op("act", lambda e, c=c, tj=tj: e.activation(out=tmp[tj], in_=ycs[c], func=AF.Square), reads=[("yc", yb, c)], writes=[("tmp", tj)])
                    yield
                    if lag2[0] is not None:
                        lag2[0]()
                    lag2[0] = (lambda c=c, tj=tj: P.op("pe", lambda e: e.matmul(psb[6][:, :], lhsT=onesb[:, :], rhs=tmp[tj], start=(c == 0), stop=(c == 7)), reads=[("tmp", tj)], writes=[PS(6)]))
                yield
                lag2[0]()
                lag2[0] = None
                yield
                P.op("act", lambda e: e.activation(out=m2, in_=psb[6][:, :], func=AF.Ln, scale=1.0 / 1024, bias=epsc[:, 0:1]), writes=[PS(6), "m2"])
                P.op("act", lambda e: e.activation(out=m2, in_=m2, func=AF.Exp, scale=-0.5), writes=["m2"])
                for c in range(8):
                    P.op("dve", lambda e, c=c: e.scalar_tensor_tensor(out=yT[:, c, t0:t0 + 512], in0=ycs[c], scalar=pv[:, 56 + c:57 + c], in1=m2, op0=ALU.mult, op1=ALU.mult), reads=[("yc", yb, c), "m2"], writes=[("yTc", c, qd)])
                    if c % 2 == 1:
                        yield

            def interleave(gm, gf, ratio=4):
                dm = df = False
                while not (dm and df):
                    if not dm:
                        try:
                            next(gm)
                        except StopIteration:
                            dm = True
                    for _ in range(ratio):
                        if not df:
                            try:
                                next(gf)
                            except StopIteration:
                                df = True

            load_w(0, 0)
            load_w(1, 1024)
            for _ in main_gen(0):
                pass
            for qd in range(1, 4):
                interleave(main_gen(qd), fin_gen(qd - 1))
            for _ in fin_gen(3):
                pass
            P.barrier()

        phase_conv()
        if stage == "B1":
            return dump(yT)

        def phase_attn():
            ar = Arena()
            o_w = [ar.bf(2048) for _ in range(6)]
            o_q = [ar.bf(2048) for _ in range(2)]
            o_k = [ar.bf(2048) for _ in range(2)]
            o_v = [ar.bf(2048) for _ in range(2)]
            o_stage = ar.f32(640)
            o_eb = ar.f32(640)
            o_q32 = [ar.f32(512) for _ in range(2)]
            o_sq = [ar.bf(512) for _ in range(2)]
            o_rs = ar.f32(512)
            o_ef = [ar.f32(640) for _ in range(3)]
            o_E = [ar.bf(640) for _ in range(3)]
            o_rz = ar.f32(128)
            o_o32 = ar.f32(128)
            o_osq = [ar.bf(128), ar.bf(128), ar.bf(128)]
            osqs = [R2b[:, o:o + 128] for o in o_osq]
            wsl = [R2b[:, o:o + 2048].rearrange("p (k c) -> p k c", c=128) for o in o_w]
            qTs = [R2b[:, o:o + 2048] for o in o_q]
            kTs = [R2b[:, o:o + 2048] for o in o_k]
            Vhs = [R2b[:, o:o + 2048].rearrange("p (t c) -> p t c", c=128) for o in o_v]
            stg = R2[:, o_stage:o_stage + 640]
            expB = R2[:, o_eb:o_eb + 640]
            q32 = [R2[:, o:o + 512] for o in o_q32]
            sq = [R2b[:, o:o + 512] for o in o_sq]
            rs = R2[:, o_rs:o_rs + 512]
            ef = [R2[:, o:o + 640] for o in o_ef]
            EB = [R2b[:, o:o + 640] for o in o_E]
            rz = R2[:, o_rz:o_rz + 128]
            o32 = R2[:, o_o32:o_o32 + 128]
            scale = 128.0 ** -0.5

            def load_w(slot, col0):
                for q in range(4):
                    P.dma("pool", f"aw{slot}", lambda e, q=q: e.dma_start(out=wsl[slot][:, 4 * q:4 * q + 4, :], in_=win_d.ap()[512 * q:512 * q + 512, col0:col0 + 128].rearrange("(k p) c -> p k c", p=128)), writes=[("aw", slot, q)])

            def load_head(h):
                s0 = 3 * (h % 2)
                load_w(s0, 2048 + h * 128)
                load_w(s0 + 1, 3072 + h * 128)
                load_w(s0 + 2, 4096 + h * 128)

            cnt = [0]

            def proj_gen(h):
                hp = h % 2
                s0 = 3 * hp
                qT, kT, Vh = qTs[hp], kTs[hp], Vhs[hp]
                pending = None
                for which, dst, gcol in ((0, qT, 72), (1, kT, 73)):
                    slot = s0 + which
                    for tb in range(4):
                        t0 = tb * 512
                        bq = cnt[0] % 2
                        cnt[0] += 1
                        for k in range(16):
                            P.op("pe", lambda e, k=k, bq=bq, slot=slot, t0=t0: e.matmul(psb[bq][:, :], lhsT=wsl[slot][:, k, :], rhs=hT[:, k, t0:t0 + 512], start=(k == 0), stop=(k == 15)), reads=[("aw", slot, q) for q in (3, 2, 1, 0)], writes=[PS(bq)])
                        P.op("act", lambda e, bq=bq: e.activation(out=q32[bq], in_=psb[bq][:, :], func=AF.Identity), writes=[PS(bq), ("q32", bq)])
                        P.op("act", lambda e, bq=bq: e.activation(out=sq[bq], in_=q32[bq], func=AF.Square), reads=[("q32", bq)], writes=[("sq", bq)])
                        if pending is not None:
                            pending()

                        def stage2(bq=bq, dst=dst, gcol=gcol, t0=t0, which=which):
                            P.op("pe", lambda e: e.matmul(psb[2][:, :], lhsT=onesb[:, :], rhs=sq[bq], start=True, stop=True), reads=[("sq", bq)], writes=[PS(2)])
                            P.op("act", lambda e: e.activation(out=rs, in_=psb[2][:, :], func=AF.Ln, scale=1.0 / 128, bias=epsc[:, 0:1]), writes=[PS(2), "rs"])
                            P.op("act", lambda e: e.activation(out=rs, in_=rs, func=AF.Exp, scale=-0.5), writes=["rs"])
                            P.op("dve", lambda e: e.scalar_tensor_tensor(out=dst[:, t0:t0 + 512], in0=q32[bq], scalar=pv[:, gcol:gcol + 1], in1=rs, op0=ALU.mult, op1=ALU.mult), reads=[("q32", bq), "rs"], writes=[("qk", hp, which)])
                        pending = stage2
                        yield
                for tg in range(4):
                    bV = cnt[0] % 2
                    cnt[0] += 1
                    pV = psb[bV][:, :].rearrange("p (j f) -> p j f", f=128)
                    for j in range(4):
                        tt = tg * 4 + j
                        for k in range(16):
                            P.op("pe", lambda e, k=k, j=j, tt=tt, pV=pV, s0=s0: e.matmul(pV[:, j, :], lhsT=hT[:, k, tt * 128:(tt + 1) * 128], rhs=wsl[s0 + 2][:, k, :], start=(k == 0), stop=(k == 15)), reads=[("aw", s0 + 2, q) for q in (3, 2, 1, 0)], writes=[PS(bV)])
                        if pending is not None:
                            pending()
                            pending = None
                        if j == 3:
                            P.op("act", lambda e, tg=tg, pV=pV, Vh=Vh, bV=bV: e.activation(out=Vh[:, tg * 4:(tg + 1) * 4, :], in_=pV, func=AF.Identity), writes=[PS(bV), ("Vh", hp)])
                        if j % 2 == 1:
                            yield

            def attn_gen(h):
                hp = h % 2
                qT, kT, Vh = qTs[hp], kTs[hp], Vhs[hp]
                P.dma("sp", "relb", lambda e, h=h: e.dma_start(out=stg, in_=relb_d.ap()[h]), writes=["stg"])
                P.op("act", lambda e: e.activation(out=expB, in_=stg, func=AF.Exp), reads=["stg"], writes=["expB"])
                P.op("dve", lambda e: e.memset(expB[64:128, 0:64], 0.0), writes=["expB"])
                P.op("dve", lambda e: e.memset(expB[0:64, 576:640], 0.0), writes=["expB"])

                def scores(t):
                    nj = min(t, 4) + 1
                    wA = min(nj, 4) * 128
                    bA = 4 + (t % 2)
                    ei = t % 3
                    c6 = 128 * (t % 2)
                    for j in range(nj):
                        m = t - j
                        if j < 4:
                            P.op("pe", lambda e, j=j, m=m: e.matmul(psb[bA][:, j * 128:(j + 1) * 128], lhsT=kT[:, m * 128:(m + 1) * 128], rhs=qT[:, t * 128:(t + 1) * 128], start=True, stop=True), reads=[("qk", hp, 0), ("qk", hp, 1)], writes=[PS(bA)])
                        else:
                            P.op("pe", lambda e, m=m: e.matmul(psb[6][:, c6:c6 + 128], lhsT=kT[:, m * 128:(m + 1) * 128], rhs=qT[:, t * 128:(t + 1) * 128], start=True, stop=True), reads=[("qk", hp, 0), ("qk", hp, 1)], writes=[PS(6)])
                    P.op("act", lambda e: e.activation(out=ef[ei][:, 0:wA], in_=psb[bA][:, 0:wA], func=AF.Exp, scale=scale), writes=[PS(bA), ("ef", ei)])
                    if nj == 5:
                        P.op("act", lambda e: e.activation(out=ef[ei][:, 512:640], in_=psb[6][:, c6:c6 + 128], func=AF.Exp, scale=scale), writes=[PS(6), ("ef", ei)])
                    wE = nj * 128
                    P.op("dve", lambda e: e.tensor_tensor(out=EB[ei][:, 0:wE], in0=ef[ei][:, 0:wE], in1=expB[:, 0:wE], op=ALU.mult), reads=[("ef", ei), "expB"], writes=[("EB", ei)])

                def pvstep(t):
                    nj = min(t, 4) + 1
                    ei = t % 3
                    b7 = 7 if t % 2 else 3
                    for j in range(nj):
                        m = t - j
                        P.op("pe", lambda e, j=j, m=m: e.matmul(psb[b7][:, 0:128], lhsT=Vh[:, m, :], rhs=EB[ei][:, j * 128:(j + 1) * 128], start=(j == 0), stop=(j == nj - 1)), reads=[("EB", ei), ("Vh", hp)], writes=[PS(b7)])
                    for j in range(nj):
                        P.op("pe", lambda e, j=j: e.matmul(psb[b7][:, 128:256], lhsT=onesb[:, :], rhs=EB[ei][:, j * 128:(j + 1) * 128], start=(j == 0), stop=(j == nj - 1)), reads=[("EB", ei)], writes=[PS(b7)])
                    if t >= 2:
                        ssa_mm(t - 2, b7)
                    P.op("dve", lambda e: e.reciprocal(out=rz, in_=psb[b7][:, 128:256]), writes=[PS(b7), "rz"])
                    P.op("dve", lambda e: e.tensor_tensor(out=o32, in0=psb[b7][:, 0:128], in1=rz, op=ALU.mult), reads=["rz"], writes=[PS(b7), "o32"])
                    if t >= 2:
                        ssa_add(t - 2, b7)
                    oi = t % 3
                    P.op("act", lambda e: e.activation(out=osqs[oi], in_=o32, func=AF.Square), reads=["o32"], writes=[("osq", oi)])
                    P.op("act", lambda e: e.activation(out=yT[:, 8 + h, t * 128:(t + 1) * 128], in_=o32, func=AF.Copy, scale=pv[:, 64 + h:65 + h]), reads=["o32"], writes=[("yTa", h, t)])

                def ssa_mm(t, b7):
                    oi = t % 3
                    P.op("pe", lambda e: e.matmul(psb[b7][:, 256:257], lhsT=osqs[oi], rhs=onesb[:, 0:1], start=True, stop=True), reads=[("osq", oi)], writes=[PS(b7)])

                def ssa_add(t, b7):
                    P.op("dve", lambda e: e.tensor_tensor(out=ssa[:, t:t + 1], in0=ssa[:, t:t + 1], in1=psb[b7][:, 256:257], op=ALU.add), writes=[PS(b7), ("ssa", t)])

                scores(0)
                yield
                for t in range(NT):
                    if t + 1 < NT:
                        scores(t + 1)
                    pvstep(t)
                    yield
                for t in (NT - 2, NT - 1):
                    ssa_mm(t, 7)
                    ssa_add(t, 7)
                yield

            def drain(g):
                for _ in g:
                    pass

            def prefetch_wout():
                dst = R2b[:, 0:8192].rearrange("p (k c) -> p k c", c=512)
                for q in range(4):
                    wr = [("owp", q)]
                    if q == 0:
                        wr += [("aw", s_, q_) for s_ in range(4) for q_ in range(4)]
                    P.dma("pool", "ow0", lambda e, q=q: e.dma_start(out=dst[:, 4 * q:4 * q + 4, :], in_=wout_d.ap()[512 * q:512 * q + 512, 0:512].rearrange("(k p) c -> p k c", p=128)), writes=wr)

            load_head(0)
            load_head(1)
            drain(proj_gen(0))
            for h in range(8):
                if h == 7:
                    prefetch_wout()
                ga = attn_gen(h)
                gp = proj_gen(h + 1) if h + 1 < 8 else None
                if h + 2 < 8:
                    load_head(h + 2)
                da = dp = False
                while not (da and (dp or gp is None)):
                    if gp is not None and not dp:
                        try:
                            next(gp)
                        except StopIteration:
                            dp = True
                    if not da:
                        try:
                            next(ga)
                        except StopIteration:
                            da = True
            P.barrier()

        phase_attn()
        P.op("act", lambda e: e.activation(out=tcol[:, :], in_=ssa[:, :], func=AF.Ln, scale=1.0 / 1024, bias=epsc[:, 0:1]), writes=["tcol"])
        P.op("act", lambda e: e.activation(out=ra[:, :], in_=tcol[:, :], func=AF.Exp, scale=-0.5), reads=["tcol"], writes=["ra"])
        P.barrier()
        if stage == "B2":
            return dump(yT, ra)

        def phase_out():
            ar = Arena()
            o_w = [ar.bf(8192) for _ in range(2)]
            o_x = [ar.f32(512) for _ in range(3)]
            o_y = [ar.f32(512) for _ in range(3)]
            wsl = [R2b[:, o:o + 8192].rearrange("p (k c) -> p k c", c=512) for o in o_w]
            xs = [R2[:, o:o + 512] for o in o_x]
            ys = [R2[:, o:o + 512] for o in o_y]

            def load_w(cb):
                s = cb % 2
                for q in range(4):
                    P.dma("pool", f"ow{s}", lambda e, q=q: e.dma_start(out=wsl[s][:, 4 * q:4 * q + 4, :], in_=wout_d.ap()[512 * q:512 * q + 512, cb * 512:(cb + 1) * 512].rearrange("(k p) c -> p k c", p=128)), writes=[("ow", s, q)])

            n = 0
            tiles = [(cb, tt) for cb in range(4) for tt in range(NT)]

            def load_x(i):
                if i >= len(tiles):
                    return
                cb, tt = tiles[i]
                xi = i % 3
                P.dma("sp", f"ox{xi}", lambda e: e.dma_start(out=xs[xi], in_=x_d.ap()[tt * 128:(tt + 1) * 128, cb * 512:(cb + 1) * 512]), writes=[("ox", xi)])

            load_x(0)
            load_x(1)
            for cb in range(4):
                if cb + 1 < 4:
                    load_w(cb + 1)
                s = cb % 2
                for tt in range(NT):
                    xi = n % 3
                    bc = n % 2
                    ba = 2 + n % 2
                    load_x(n + 2)
                    n += 1
                    for k in range(8):
                        P.op("pe", lambda e, k=k, tt=tt, s=s, bc=bc: e.matmul(psb[bc][:, :], lhsT=yT[:, k, tt * 128:(tt + 1) * 128], rhs=wsl[s][:, k, :], start=(k == 0), stop=(k == 7)), reads=[("ow", s, q) for q in (3, 2, 1, 0)], writes=[PS(bc)])
                    for k in range(8, 16):
                        P.op("pe", lambda e, k=k, tt=tt, s=s, ba=ba: e.matmul(psb[ba][:, :], lhsT=yT[:, k, tt * 128:(tt + 1) * 128], rhs=wsl[s][:, k, :], start=(k == 8), stop=(k == 15)), reads=[("ow", s, q) for q in (3, 2, 1, 0)], writes=[PS(ba)])
                    P.op("dve", lambda e, bc=bc, xi=xi: e.tensor_tensor(out=ys[xi], in0=psb[bc][:, :], in1=xs[xi], op=ALU.add), reads=[("ox", xi)], writes=[PS(bc), ("oy", xi)])
                    P.op("dve", lambda e, ba=ba, xi=xi, tt=tt: e.scalar_tensor_tensor(out=ys[xi], in0=psb[ba][:, :], scalar=ra[:, tt:tt + 1], in1=ys[xi], op0=ALU.mult, op1=ALU.add), writes=[PS(ba), ("oy", xi)])
                    P.dma("sp", f"oy{xi}", lambda e, tt=tt, cb=cb, xi=xi: e.dma_start(out=out_d.ap()[tt * 128:(tt + 1) * 128, cb * 512:(cb + 1) * 512], in_=ys[xi]), reads=[("oy", xi)])
            P.barrier()

        phase_out()
        if stage == "C":
            return dump(yT, ra)

        phase_norm_T(out_d, 16, hT, "A2")
        if stage == "A2":
            return dump(hT, rcol)

        def phase_ffn():
            ar = Arena()
            o_f = ar.bf(8 * 1024)
            o_w1 = [ar.bf(16 * 256) for _ in range(3)]
            o_w2 = [ar.bf(8 * 512) for _ in range(3)]
            o_r = [ar.f32(512) for _ in range(2)]
            fT = R2b[:, o_f:o_f + 8192].rearrange("p (j t) -> p j t", t=1024)
            w1s = [R2b[:, o:o + 4096].rearrange("p (k c) -> p k c", c=256) for o in o_w1]
            w2s = [R2b[:, o:o + 4096].rearrange("p (j c) -> p j c", c=512) for o in o_w2]
            rl = [R2[:, o:o + 512] for o in o_r]
            acc = R1f
            n1 = [0]
            n2 = [0]
            w1q = []
            w2q = []
            for hf in range(2):
                for g in range(8):
                    for jj in range(4):
                        w1q.append((hf, g, jj))
                    for cb in range(4):
                        w2q.append((hf, g, cb))
            w1i = [0]
            w2i = [0]

            def issue_w1():
                if w1i[0] >= len(w1q):
                    return
                hf, g, jj = w1q[w1i[0]]
                s = w1i[0] % 3
                w1i[0] += 1
                c0 = g * 1024 + jj * 256
                for q in range(4):
                    P.dma("pool", f"w1{s}", lambda e, q=q: e.dma_start(out=w1s[s][:, 4 * q:4 * q + 4, :], in_=w1_d.ap()[512 * q:512 * q + 512, c0:c0 + 256].rearrange("(k p) c -> p k c", p=128)), writes=[("w1", s, q)])

            def issue_w2():
                if w2i[0] >= len(w2q):
                    return
                hf, g, cb = w2q[w2i[0]]
                s = w2i[0] % 3
                w2i[0] += 1
                for q in range(2):
                    P.dma("pool", f"w2{s}", lambda e, q=q: e.dma_start(out=w2s[s][:, 4 * q:4 * q + 4, :], in_=w2_d.ap()[g * 1024 + 512 * q:g * 1024 + 512 * q + 512, cb * 512:(cb + 1) * 512].rearrange("(j p) c -> p j c", p=128)), writes=[("w2", s, q)])

            issue_w1()
            issue_w1()
            issue_w2()
            issue_w2()
            u1 = 0
            u2 = 0
            for hf in range(2):
                for q in range(4):
                    P.dma("sp", "acc", lambda e, hf=hf, q=q: e.dma_start(out=acc[:, 2 * q:2 * q + 2, :], in_=out_d.ap()[hf * 1024 + 256 * q:hf * 1024 + 256 * q + 256, :].rearrange("(t p) c -> p t c", p=128)), writes=[("accl", q), ("accw", q)])
                for g in range(8):
                    for jj in range(4):
                        issue_w1()
                        s = u1 % 3
                        u1 += 1
                        for j2 in range(2):
                            j = jj * 2 + j2
                            for tb in range(2):
                                t0 = hf * 1024 + tb * 512
                                b = n1[0] % 4
                                ri = n1[0] % 2
                                n1[0] += 1
                                for k in range(16):
                                    P.op("pe", lambda e, k=k, b=b, s=s, j2=j2, t0=t0: e.matmul(psb[b][:, :], lhsT=w1s[s][:, k, j2 * 128:(j2 + 1) * 128], rhs=hT[:, k, t0:t0 + 512], start=(k == 0), stop=(k == 15)), reads=[("w1", s, q) for q in (3, 2, 1, 0)], writes=[PS(b)])
                                P.op("act", lambda e, b=b, ri=ri: e.activation(out=rl[ri], in_=psb[b][:, :], func=AF.Relu), writes=[PS(b), ("rl", ri)])
                                P.op("act", lambda e, ri=ri, j=j, tb=tb: e.activation(out=fT[:, j, tb * 512:(tb + 1) * 512], in_=rl[ri], func=AF.Square), reads=[("rl", ri)], writes=[("fT", j)])
                    for cb in range(4):
                        issue_w2()
                        s = u2 % 3
                        u2 += 1
                        for tt in range(8):
                            b = 4 + n2[0] % 4
                            n2[0] += 1
                            for j in range(8):
                                P.op("pe", lambda e, j=j, b=b, s=s, tt=tt: e.matmul(psb[b][:, :], lhsT=fT[:, j, tt * 128:(tt + 1) * 128], rhs=w2s[s][:, j, :], start=(j == 0), stop=(j == 7)), reads=[("w2", s, 1), ("w2", s, 0), ("fT", j)], writes=[PS(b)])
                            P.op("dve", lambda e, b=b, tt=tt, cb=cb: e.tensor_tensor(out=acc[:, tt, cb * 512:(cb + 1) * 512], in0=acc[:, tt, cb * 512:(cb + 1) * 512], in1=psb[b][:, :], op=ALU.add), reads=[("accl", tt // 2)], writes=[PS(b), ("acc_t", tt, cb)])
                for q in range(4):
                    P.dma("sp", "accout", lambda e, hf=hf, q=q: e.dma_start(out=out_d.ap()[hf * 1024 + 256 * q:hf * 1024 + 256 * q + 256, :].rearrange("(t p) c -> p t c", p=128), in_=acc[:, 2 * q:2 * q + 2, :]), reads=[("acc_t", tt, cb) for tt in range(8) for cb in range(4)], writes=[("accw", q)])
                if hf == 1:
                    P.barrier()

        phase_ffn()
        P.wait_all_dma("sp")
        P.emit(nc)
    return nc


def _prep_shared(inputs):
    f = lambda k: np.ascontiguousarray(np.asarray(inputs[k], dtype=np.float32)[0])
    pvec = np.zeros((128, NPV), np.float32)
    pvec[:, 0:16] = f("ln1_g").reshape(16, 128).T
    pvec[:, 16:32] = f("ln2_g").reshape(16, 128).T
    pvec[:, 32:40] = f("b_dw").reshape(8, 128).T
    pvec[:, 40:48] = f("conv_ln_g").reshape(8, 128).T
    pvec[:, 48:56] = f("conv_ln_b").reshape(8, 128).T
    pvec[:, 56:64] = f("out_norm_conv_g").reshape(8, 128).T
    pvec[:, 64:72] = f("out_norm_attn_g").reshape(8, 128).T
    pvec[:, 72] = f("q_norm_g")
    pvec[:, 73] = f("k_norm_g")
    wdw = f("w_dw")
    pvec[:, 74:] = wdw.reshape(31, 8, 128).transpose(2, 1, 0).reshape(128, 248)
    rel = np.arange(640)[None, :] - np.arange(128)[:, None]
    idx = np.clip(rel, -128, 128) + 128
    relb = np.ascontiguousarray(f("rel_bias")[:, idx])
    return {
        "w_in": f("w_in"), "w_out": f("w_out"), "w_ff1": f("w_ff1"), "w_ff2": f("w_ff2"),
        "pvec": pvec, "relb": relb, "ident": np.eye(128, dtype=np.float32),
    }


def kernel(**inputs):
    x = np.asarray(inputs["x"], dtype=np.float32)
    shared = _prep_shared(inputs)
    nc = build_program("full")
    in_maps = [dict(shared, x=np.ascontiguousarray(x[b])) for b in range(8)]
    res = run_bass_kernel_spmd(nc, in_maps, core_ids=list(range(8)))
    return np.stack([np.asarray(r["out"], dtype=np.float32) for r in res.results], axis=0)
```
